# Optimizing a Trainium2 kernel written in Bass

```python
import math
import jax, jax.numpy as jnp
from jax import lax
import numpy as np

D_MODEL = 1024
BATCH = 4
SEQ = 4096
DEPTH = 4

HEAD_DIM = 64
H_A = 8
KV_RANK = 128
H_IDX = 8
D_IDX = 64
TOPK_MAX = 256
H_B = 8
Q_BLOCK = 128
N_BUCKETS = 32
T5_MAX_DIST = 128
G_C = 8
CHUNK = 128
C_WIDTH = G_C * HEAD_DIM
G_D = 8
D_WIDTH = G_D * HEAD_DIM
CONV_W = 3
D_FF = 2816
N_EVEN = (DEPTH + 1) // 2
N_ODD = DEPTH // 2
ALPHA = (2.0 * DEPTH) ** 0.25
BETA = (8.0 * DEPTH) ** -0.25
LN_EPS = 1e-5
EVEN_WIDTHS = (H_A * HEAD_DIM, KV_RANK, H_IDX * D_IDX, D_IDX, H_IDX,
               H_B * HEAD_DIM, H_B * HEAD_DIM, H_B * HEAD_DIM, H_B)
ODD_WIDTHS = (C_WIDTH, C_WIDTH, D_WIDTH, D_WIDTH, D_WIDTH)
EVEN_PROJ = sum(EVEN_WIDTHS)
ODD_PROJ = sum(ODD_WIDTHS)
EVEN_MIX = H_A * HEAD_DIM + H_B * HEAD_DIM
ODD_MIX = C_WIDTH + D_WIDTH

kernel_name = "hybrid_dsa_fox_gmlp_shortconv_trunk"


def _split(h, widths):
    out, off = [], 0
    for w in widths:
        out.append(h[..., off:off + w])
        off += w
    return out


def layer_norm(x, g, b=None):
    xf = x.astype(jnp.float32)
    mu = jnp.mean(xf, axis=-1, keepdims=True)
    xc = xf - mu
    var = jnp.mean(xc * xc, axis=-1, keepdims=True)
    y = xc * lax.rsqrt(var + LN_EPS) * g.astype(jnp.float32)
    if b is not None:
        y = y + b.astype(jnp.float32)
    return y.astype(x.dtype)


def causal_dwconv(z, w):
    width, ch = w.shape
    return lax.conv_general_dilated(
        z, w[:, None, :].astype(z.dtype), window_strides=(1,),
        padding=[(width - 1, 0)], dimension_numbers=('NWC', 'WIO', 'NWC'),
        feature_group_count=ch)


def t5_bucket(dist):
    max_exact = N_BUCKETS // 2
    n = jnp.maximum(dist, 0)
    nf = jnp.maximum(n, 1).astype(jnp.float32)
    large = max_exact + (jnp.log(nf / max_exact) / math.log(T5_MAX_DIST / max_exact)
                         * (N_BUCKETS - max_exact)).astype(jnp.int32)
    large = jnp.minimum(large, N_BUCKETS - 1)
    return jnp.where(n < max_exact, n, large)


def to_blocks(a):
    return jnp.moveaxis(a.reshape(a.shape[0], -1, Q_BLOCK, *a.shape[2:]), 1, 0)


def from_blocks(a):
    a = jnp.moveaxis(a, 0, 1)
    return a.reshape(a.shape[0], -1, *a.shape[3:])


def even_mixer(x, w_in, w_uk, w_uv, b_f, w_o, rel_bias):
    bsz, seq, _ = x.shape
    k_sel = min(TOPK_MAX, seq // 4)
    n_blk = seq // Q_BLOCK
    h = x @ w_in
    q_a, c_kv, q_idx, k_idx, w_idx, q_b, k_b, v_b, f_in = _split(h, EVEN_WIDTHS)
    q_a = q_a.reshape(bsz, seq, H_A, HEAD_DIM)
    q_lat = jnp.einsum('bshd,hrd->bshr', q_a, w_uk) * (HEAD_DIM ** -0.5)
    q_idx = q_idx.reshape(bsz, seq, H_IDX, D_IDX) * (D_IDX ** -0.5)
    w_idx = w_idx * (H_IDX ** -0.5)
    q_b = q_b.reshape(bsz, seq, H_B, HEAD_DIM) * (HEAD_DIM ** -0.5)
    k_b = k_b.reshape(bsz, seq, H_B, HEAD_DIM)
    v_b = v_b.reshape(bsz, seq, H_B, HEAD_DIM)
    log_f = jax.nn.log_sigmoid(f_in.astype(jnp.float32) + b_f.astype(jnp.float32))
    cum_f = jnp.cumsum(log_f, axis=1)
    cum_f_keys = jnp.swapaxes(cum_f, 1, 2)

    key_pos = jnp.arange(seq)
    bidx = jnp.arange(bsz)[:, None, None]

    def block(args):
        i, ql, qi, wi, qb, fq = args
        q_pos = i * Q_BLOCK + jnp.arange(Q_BLOCK)
        causal = key_pos[None, :] <= q_pos[:, None]
        idx = jnp.einsum('bthd,bsd->bths', qi, k_idx)
        score = jnp.einsum('bth,bths->bts', wi, jax.nn.relu(idx)).astype(jnp.float32)
        score = jnp.where(causal[None], score, -jnp.inf)
        _, sel = lax.top_k(score, k_sel)
        valid = sel <= q_pos[None, :, None]
        c_sel = c_kv[bidx, sel]
        la = jnp.einsum('bthr,btkr->bhtk', ql, c_sel).astype(jnp.float32)
        bias = rel_bias[t5_bucket(q_pos[None, :, None] - sel)]
        la = la + jnp.moveaxis(bias, -1, 1).astype(jnp.float32)
        la = jnp.where(valid[:, None], la, -jnp.inf)
        pa = jax.nn.softmax(la, axis=-1).astype(c_sel.dtype)
        oa = jnp.einsum('bhtk,btkr->bthr', pa, c_sel)
        lb = jnp.einsum('bthd,bshd->bhts', qb, k_b).astype(jnp.float32)
        lb = lb + jnp.swapaxes(fq, 1, 2)[..., None] - cum_f_keys[:, :, None, :]
        lb = jnp.where(causal[None, None], lb, -jnp.inf)
        pb = jax.nn.softmax(lb, axis=-1).astype(v_b.dtype)
        ob = jnp.einsum('bhts,bshd->bthd', pb, v_b)
        return oa, ob

    oa, ob = lax.map(block, (jnp.arange(n_blk), to_blocks(q_lat), to_blocks(q_idx),
                             to_blocks(w_idx), to_blocks(q_b), to_blocks(cum_f)))
    o_a = jnp.einsum('bshr,hrd->bshd', from_blocks(oa), w_uv).reshape(bsz, seq, H_A * HEAD_DIM)
    o_b = from_blocks(ob).reshape(bsz, seq, H_B * HEAD_DIM)
    return jnp.concatenate([o_a, o_b], axis=-1) @ w_o


def odd_mixer(x, w_in, sgu_g, sgu_w, sgu_b, conv_w, w_o):
    bsz, seq, _ = x.shape
    h = x @ w_in
    u, v, g_b, g_c, z = _split(h, ODD_WIDTHS)
    u = jax.nn.gelu(u)
    v = layer_norm(jax.nn.gelu(v).reshape(bsz, seq, G_C, HEAD_DIM), sgu_g)
    v = v.reshape(bsz, seq // CHUNK, CHUNK, G_C, HEAD_DIM)
    mix = jnp.einsum('gts,bnsgd->bntgd', jnp.tril(sgu_w), v) + jnp.swapaxes(sgu_b, 0, 1)[:, :, None]
    o_c = u * mix.reshape(bsz, seq, C_WIDTH)
    o_d = g_b * causal_dwconv(g_c * z, conv_w)
    return jnp.concatenate([o_c, o_d], axis=-1) @ w_o


def conv_ffn(x, w_up, conv_w, w_down):
    h = causal_dwconv(x @ w_up, conv_w)
    g, v = jnp.split(h, 2, axis=-1)
    return (jax.nn.silu(g) * v) @ w_down


def setup_inputs(seed: int = 0) -> dict:
    key = jax.random.key(seed)
    ks = jax.random.split(key, 18)

    def nrm(k, shape, scale):
        return jax.random.normal(k, shape, jnp.float32) * scale

    return {
        "x": nrm(ks[0], (BATCH, SEQ, D_MODEL), 1.0),
        "ln_g": 1.0 + nrm(ks[1], (DEPTH, 2, D_MODEL), 0.02),
        "ln_b": nrm(ks[2], (DEPTH, 2, D_MODEL), 0.02),
        "rel_bias": nrm(ks[3], (N_BUCKETS, H_A), 0.5),
        "ev_w_in": nrm(ks[4], (N_EVEN, D_MODEL, EVEN_PROJ), D_MODEL ** -0.5),
        "ev_w_uk": nrm(ks[5], (N_EVEN, H_A, KV_RANK, HEAD_DIM), KV_RANK ** -0.5),
        "ev_w_uv": nrm(ks[6], (N_EVEN, H_A, KV_RANK, HEAD_DIM), KV_RANK ** -0.5),
        "ev_b_f": jax.random.uniform(ks[7], (N_EVEN, H_B), jnp.float32, 1.0, 4.0),
        "ev_w_o": nrm(ks[8], (N_EVEN, EVEN_MIX, D_MODEL), BETA * EVEN_MIX ** -0.5),
        "od_w_in": nrm(ks[9], (N_ODD, D_MODEL, ODD_PROJ), D_MODEL ** -0.5),
        "od_sgu_g": 1.0 + nrm(ks[10], (N_ODD, G_C, HEAD_DIM), 0.02),
        "od_sgu_w": nrm(ks[11], (N_ODD, G_C, CHUNK, CHUNK), CHUNK ** -0.5),
        "od_sgu_b": 1.0 + nrm(ks[12], (N_ODD, G_C, CHUNK), 0.1),
        "od_conv_w": nrm(ks[13], (N_ODD, CONV_W, D_WIDTH), CONV_W ** -0.5),
        "od_w_o": nrm(ks[14], (N_ODD, ODD_MIX, D_MODEL), BETA * ODD_MIX ** -0.5),
        "ffn_w_up": nrm(ks[15], (DEPTH, D_MODEL, 2 * D_FF), D_MODEL ** -0.5),
        "ffn_conv_w": nrm(ks[16], (DEPTH, CONV_W, 2 * D_FF), CONV_W ** -0.5),
        "ffn_w_down": nrm(ks[17], (DEPTH, D_FF, D_MODEL), BETA * D_FF ** -0.5),
    }


def reference(x, ln_g, ln_b, rel_bias, ev_w_in, ev_w_uk, ev_w_uv, ev_b_f, ev_w_o,
              od_w_in, od_sgu_g, od_sgu_w, od_sgu_b, od_conv_w, od_w_o,
              ffn_w_up, ffn_conv_w, ffn_w_down):
    for layer in range(DEPTH):
        j = layer // 2
        if layer % 2 == 0:
            m = even_mixer(x, ev_w_in[j], ev_w_uk[j], ev_w_uv[j], ev_b_f[j], ev_w_o[j], rel_bias)
        else:
            m = odd_mixer(x, od_w_in[j], od_sgu_g[j], od_sgu_w[j], od_sgu_b[j], od_conv_w[j], od_w_o[j])
        x = layer_norm(ALPHA * x + m, ln_g[layer, 0], ln_b[layer, 0])
        f = conv_ffn(x, ffn_w_up[layer], ffn_conv_w[layer], ffn_w_down[layer])
        x = layer_norm(ALPHA * x + f, ln_g[layer, 1], ln_b[layer, 1])
    return x
```

```python
import numpy as np
import ml_dtypes
from contextlib import ExitStack
import concourse.bass as bass
import concourse.mybir as mybir
from concourse.bass_utils import run_bass_kernel_spmd

F32 = mybir.dt.float32
BF16 = mybir.dt.bfloat16
ALU = mybir.AluOpType
AF = mybir.ActivationFunctionType
AX = mybir.AxisListType

D = 1024
T = 2048
NT = 16
DFF = 2816
DEPTH = 4
ALPHA = (2.0 * DEPTH) ** 0.25
LN_EPS = 1e-5
NEG = -30000.0
NEGF = -1.0e30

ENGS = ("pe", "act", "dve", "pool", "sp")
NDMA = 32


class Res:
    __slots__ = ("name", "w", "r")

    def __init__(self, name=""):
        self.name = name
        self.w = None
        self.r = {}


class KB:
    def __init__(self):
        self.nc = bass.Bass("TRN2", target_bir_lowering=False)
        self.es = ExitStack()
        nc = self.nc
        self.sem = {}
        for e in ENGS:
            self.sem[e] = self.es.enter_context(nc.semaphore("s_" + e))
        self.dsem = [self.es.enter_context(nc.semaphore("d%d" % i)) for i in range(NDMA)]
        self.dval = [0] * NDMA
        self.dnext = 0
        self.cnt = {e: 0 for e in ENGS}
        self.seen = {e: {} for e in ENGS}
        self.prog = {e: [] for e in ENGS}
        self.pes = []
        self.uid = 0

    def _stack(self):
        return self.pes[-1] if self.pes else self.es

    def sbuf(self, name, shape, dt):
        self.uid += 1
        return self._stack().enter_context(
            self.nc.sbuf_tensor("%s_%d" % (name, self.uid), list(shape), dt))

    def psum(self, name, shape, dt=F32):
        self.uid += 1
        return self._stack().enter_context(
            self.nc.psum_tensor("%s_%d" % (name, self.uid), list(shape), dt))

    def dram(self, name, shape, dt, kind="Internal"):
        return self.nc.dram_tensor(name, list(shape), dt, kind=kind)

    def res(self, name=""):
        return Res(name)

    def begin_phase(self):
        self.pes.append(ExitStack())

    def end_phase(self):
        self.barrier()
        self.pes.pop().close()

    def barrier(self):
        evs = [(p, self.cnt[p]) for p in ENGS if self.cnt[p] > 0]
        evs += [(s, self.dval[s]) for s in range(NDMA) if self.dval[s] > 0]
        for e in ENGS:
            for ev in evs:
                if ev[0] != e:
                    self._wait(e, ev)

    def _semh(self, key):
        return self.sem[key] if isinstance(key, str) else self.dsem[key]

    def _wait(self, e, ev):
        key, val = ev
        if self.seen[e].get(key, 0) >= val:
            return
        self.seen[e][key] = val
        self.prog[e].append(("w", key, val))

    def _deps(self, e, r, w):
        evs = []
        for x in r:
            if x.w is not None:
                evs.append(x.w)
        for x in w:
            if x.w is not None:
                evs.append(x.w)
            evs.extend(x.r.items())
        for ev in evs:
            if ev[0] == e and e == "pe":
                continue
            self._wait(e, ev)

    def _mark(self, ev, r, w):
        for x in r:
            if x.r.get(ev[0], 0) < ev[1]:
                x.r[ev[0]] = ev[1]
        for x in w:
            x.w = ev
            x.r = {}

    def op(self, e, fn, r=(), w=()):
        self._deps(e, r, w)
        self.cnt[e] += 1
        ev = (e, self.cnt[e])
        self.prog[e].append(("i", fn, e, 1))
        self._mark(ev, r, w)
        return ev

    def dma(self, q, out, in_, r=(), w=(), **kw):
        self._deps(q, r, w)
        slot = self.dnext
        self.dnext = (self.dnext + 1) % NDMA
        if self.dval[slot] > 0:
            self._wait(q, (slot, self.dval[slot]))
        self.dval[slot] += 16
        ev = (slot, self.dval[slot])
        self.prog[q].append(("i", lambda en: en.dma_start(out=out, in_=in_, **kw), slot, 16))
        self._mark(ev, r, w)
        return ev

    def finish(self):
        self.barrier()
        nc = self.nc
        prog = self.prog
        kb = self

        def replay(e):
            def run(en):
                for it in prog[e]:
                    if it[0] == "w":
                        en.wait_ge(kb._semh(it[1]), it[2])
                    else:
                        it[1](en).then_inc(kb._semh(it[2]), it[3])
            return run

        with nc.Block() as block:
            block.tensor(replay("pe"))
            block.scalar(replay("act"))
            block.vector(replay("dve"))
            block.gpsimd(replay("pool"))
            block.sync(replay("sp"))
        self.es.close()
        return nc


class Common:
    def __init__(self, kb, cst):
        self.kb = kb
        self.id16 = kb.sbuf("id16", [128, 128], BF16)
        self.id32 = kb.sbuf("id32", [128, 128], F32)
        self.r_id = kb.res("id")
        kb.dma("sp", self.id32[:], cst["ident"], w=[self.r_id])
        kb.op("dve", lambda e: e.tensor_copy(out=self.id16[:], in_=self.id32[:]),
              r=[self.r_id], w=[self.r_id])


def cast_copy(kb, eng, out, in_, r, w, scale=None):
    if eng == "act":
        if scale is None:
            return kb.op("act", lambda e: e.copy(out=out, in_=in_), r=r, w=w)
        return kb.op("act", lambda e: e.mul(out=out, in_=in_, mul=scale), r=r, w=w)
    if scale is None:
        return kb.op(eng, lambda e: e.tensor_copy(out=out, in_=in_), r=r, w=w)
    return kb.op(eng, lambda e: e.tensor_scalar(out=out, in0=in_, scalar1=scale, scalar2=None,
                                                op0=ALU.mult), r=r, w=w)


def build_xT(kb, cm, x_dram, halo_dram, xT, r_xT):
    kb.begin_phase()
    xs32 = [kb.sbuf("xs32", [128, D], F32) for _ in range(2)]
    xs16 = [kb.sbuf("xs16", [128, D], BF16) for _ in range(2)]
    pst = [kb.psum("pst", [128, D], BF16) for _ in range(2)]
    r32 = [kb.res() for _ in range(2)]
    r16 = [kb.res() for _ in range(2)]
    rps = [kb.res() for _ in range(2)]
    h32 = kb.sbuf("h32", [2, D], F32)
    h16 = kb.sbuf("h16", [2, D], BF16)
    psh = kb.psum("psh", [128, 16], BF16)
    rh = kb.res()
    kb.dma("sp", h32[:], halo_dram, w=[rh])
    kb.op("dve", lambda e: e.tensor_copy(out=h16[:], in_=h32[:]), r=[rh], w=[rh])
    for k in range(8):
        kb.op("pe", lambda e, k=k: e.transpose(out=psh[:, 2 * k:2 * k + 2],
                                               in_=h16[:, k * 128:(k + 1) * 128],
                                               identity=cm.id16[0:2, 0:2]),
              r=[rh, cm.r_id], w=[rh])
    kb.op("dve", lambda e: e.tensor_copy(out=xT[:, :, 0:2],
                                         in_=psh[:].rearrange("p (k t) -> p k t", t=2)),
          r=[rh], w=[r_xT])
    for i in range(NT):
        b = i % 2
        kb.dma("sp", xs32[b][:], x_dram[i * 128:(i + 1) * 128, :], w=[r32[b]])
        kb.op("pool", lambda e, b=b: e.tensor_copy(out=xs16[b][:], in_=xs32[b][:]),
              r=[r32[b]], w=[r16[b]])
        for k in range(8):
            kb.op("pe", lambda e, b=b, k=k: e.transpose(out=pst[b][:, k * 128:(k + 1) * 128],
                                                        in_=xs16[b][:, k * 128:(k + 1) * 128],
                                                        identity=cm.id16[:]),
                  r=[r16[b], cm.r_id], w=[rps[b]])
        eng = "act" if i % 2 == 0 else "dve"
        cast_copy(kb, eng, xT[:, :, 2 + i * 128:2 + (i + 1) * 128],
                  pst[b][:].rearrange("p (k t) -> p k t", t=128), r=[rps[b]], w=[r_xT])
    kb.end_phase()


class LNEpilogue:
    def __init__(self, kb, g_dram, b_dram):
        self.kb = kb
        self.g = kb.sbuf("lng", [128, D], F32)
        self.b = kb.sbuf("lnb", [128, D], F32)
        self.rgb = kb.res()
        kb.dma("sp", self.g[:], g_dram.partition_broadcast(128), w=[self.rgb])
        kb.dma("sp", self.b[:], b_dram.partition_broadcast(128), w=[self.rgb])
        self.xr = [kb.sbuf("xr", [128, D], F32) for _ in range(2)]
        self.y = [kb.sbuf("y", [128, D], F32) for _ in range(2)]
        self.st = [kb.sbuf("st", [128, 12], F32) for _ in range(2)]
        self.mv = [kb.sbuf("mv", [128, 2], F32) for _ in range(2)]
        self.rs = [kb.sbuf("rs", [128, 1], F32) for _ in range(2)]
        self.rxr = [kb.res() for _ in range(2)]
        self.ry = [kb.res() for _ in range(2)]
        self.rsm = [kb.res() for _ in range(2)]
        self.n = 0

    def run(self, ps, r_ps, xres_dram_rows, out_dram_rows):
        kb = self.kb
        b = self.n % 2
        self.n += 1
        xr, y, st, mv, rs = self.xr[b], self.y[b], self.st[b], self.mv[b], self.rs[b]
        rxr, ry, rsm = self.rxr[b], self.ry[b], self.rsm[b]
        kb.dma("sp", xr[:], xres_dram_rows, w=[rxr])
        kb.op("dve", lambda e: e.scalar_tensor_tensor(out=y[:], in0=xr[:], scalar=ALPHA, in1=ps,
                                                      op0=ALU.mult, op1=ALU.add),
              r=[rxr, r_ps], w=[ry])
        kb.op("dve", lambda e: e.bn_stats(out=st[:, 0:6], in_=y[:, 0:512]), r=[ry], w=[rsm])
        kb.op("dve", lambda e: e.bn_stats(out=st[:, 6:12], in_=y[:, 512:1024]), r=[ry], w=[rsm])
        kb.op("dve", lambda e: e.bn_aggr(out=mv[:], in_=st[:]), r=[rsm], w=[rsm])
        kb.op("act", lambda e: e.activation(out=rs[:], in_=mv[:, 1:2], func=AF.Sqrt, bias=LN_EPS,
                                            scale=1.0), r=[rsm], w=[rsm])
        kb.op("dve", lambda e: e.reciprocal(out=rs[:], in_=rs[:]), r=[rsm], w=[rsm])
        kb.op("dve", lambda e: e.tensor_scalar(out=y[:], in0=y[:], scalar1=mv[:, 0:1], scalar2=rs[:],
                                               op0=ALU.subtract, op1=ALU.mult), r=[ry, rsm], w=[ry])
        kb.op("pool", lambda e: e.tensor_tensor(out=y[:], in0=y[:], in1=self.g[:], op=ALU.mult),
              r=[ry, self.rgb], w=[ry])
        kb.op("pool", lambda e: e.tensor_tensor(out=y[:], in0=y[:], in1=self.b[:], op=ALU.add),
              r=[ry, self.rgb], w=[ry])
        kb.dma("sp", out_dram_rows, y[:], r=[ry])


def ffn_phase(kb, cm, io):
    kb.begin_phase()
    NC = DFF // 128
    xT = kb.sbuf("xT", [128, 8, 2 + T], BF16)
    r_xT = kb.res("xT")
    build_xT(kb, cm, io["x1"], io["halo"], xT, r_xT)

    cw = kb.sbuf("cw", [128, 132], F32)
    wd16 = kb.sbuf("wd16", [128, NC, D], BF16)
    kb.begin_phase()
    cwr = kb.sbuf("cwr", [128, 128], F32)
    cwr2 = kb.sbuf("cwr2", [4, 128], F32)
    pscw = kb.psum("pscw", [128, 132], F32)
    rcw = kb.res()
    cwv = io["conv_w"].rearrange("j (c p) -> (j c) p", p=128)
    kb.dma("sp", cwr[:], cwv[0:128, :], w=[rcw])
    kb.dma("sp", cwr2[:], cwv[128:132, :], w=[rcw])
    kb.op("pe", lambda e: e.transpose(out=pscw[:, 0:128], in_=cwr[:], identity=cm.id32[:]),
          r=[rcw, cm.r_id], w=[rcw])
    kb.op("pe", lambda e: e.transpose(out=pscw[:, 128:132], in_=cwr2[:], identity=cm.id32[0:4, 0:4]),
          r=[rcw, cm.r_id], w=[rcw])
    kb.op("dve", lambda e: e.tensor_copy(out=cw[:], in_=pscw[:]), r=[rcw], w=[rcw])

    rwd = kb.res()
    wdst = [kb.sbuf("wdst", [128, D], F32) for _ in range(2)]
    rwdst = [kb.res() for _ in range(2)]
    for c in range(NC):
        b = c % 2
        kb.dma("sp", wdst[b][:], io["w_down"][c * 128:(c + 1) * 128, :], w=[rwdst[b]])
        cast_copy(kb, "pool", wd16[:, c, :], wdst[b][:], r=[rwdst[b]], w=[rwd])
    kb.end_phase()

    ep = LNEpilogue(kb, io["ln_g"], io["ln_b"])

    wst = [[kb.sbuf("wst", [128, 8, 128], F32) for _ in range(2)] for _ in range(2)]
    w16 = [[kb.sbuf("w16", [128, 8, 128], BF16) for _ in range(2)] for _ in range(2)]
    rwst = [[kb.res() for _ in range(2)] for _ in range(2)]
    rw16 = [[kb.res() for _ in range(2)] for _ in range(2)]
    psh = [[kb.psum("psh", [128, 512], F32) for _ in range(2)] for _ in range(2)]
    rpsh = [[kb.res() for _ in range(2)] for _ in range(2)]
    hb = [[kb.sbuf("hb", [128, 514], F32) for _ in range(2)] for _ in range(2)]
    rhb = [[kb.res() for _ in range(2)] for _ in range(2)]
    cv = [[kb.sbuf("cv", [128, 512], F32) for _ in range(2)] for _ in range(2)]
    rcv = [[kb.res() for _ in range(2)] for _ in range(2)]
    stash = kb.sbuf("stash", [128, 2, NC, 2], F32)
    rstash = [[kb.res() for _ in range(NC)] for _ in range(2)]
    pshalo = kb.psum("pshalo", [128, 4], F32)
    rpshalo = kb.res()
    aT = kb.sbuf("aT", [128, NC, 1024], BF16)
    raT = kb.res()
    pso = [kb.psum("pso", [128, D], F32) for _ in range(1)]
    rpso = [kb.res() for _ in range(1)]
    w_up = io["w_up"].rearrange("(kc p) f -> p kc f", p=128)

    it = 0
    for half in range(2):
        for c in range(NC):
            wb = c % 2
            for which in range(2):
                col = which * DFF + c * 128
                kb.dma("sp", wst[which][wb][:], w_up[:, :, col:col + 128], w=[rwst[which][wb]])
                cast_copy(kb, "act" if which == 0 else "pool", w16[which][wb][:], wst[which][wb][:],
                          r=[rwst[which][wb]], w=[rw16[which][wb]])
            if half == 0:
                for which in range(2):
                    for kc in range(8):
                        kb.op("pe", lambda e, which=which, wb=wb, kc=kc: e.matmul(
                            pshalo[:, which * 2:which * 2 + 2], lhsT=w16[which][wb][:, kc, :],
                            rhs=xT[:, kc, 0:2], start=(kc == 0), stop=(kc == 7)),
                            r=[rw16[which][wb], r_xT], w=[rpshalo])
                    kb.op("dve", lambda e, which=which, c=c: e.tensor_copy(
                        out=stash[:, which, c, :], in_=pshalo[:, which * 2:which * 2 + 2]),
                        r=[rpshalo], w=[rstash[which][c]])
            for blk in range(2):
                pb = it % 2
                it += 1
                col0 = 2 + half * 1024 + blk * 512
                for which in range(2):
                    for kc in range(8):
                        kb.op("pe", lambda e, which=which, wb=wb, kc=kc, pb=pb, col0=col0: e.matmul(
                            psh[which][pb][:], lhsT=w16[which][wb][:, kc, :],
                            rhs=xT[:, kc, col0:col0 + 512], start=(kc == 0), stop=(kc == 7)),
                            r=[rw16[which][wb], r_xT], w=[rpsh[which][pb]])
                for which in range(2):
                    h = hb[which][pb]
                    rh = rhb[which][pb]
                    kb.op("pool", lambda e, h=h, which=which, c=c: e.tensor_copy(
                        out=h[:, 0:2], in_=stash[:, which, c, :]), r=[rstash[which][c]], w=[rh])
                    kb.op("act", lambda e, h=h, which=which, pb=pb: e.copy(
                        out=h[:, 2:514], in_=psh[which][pb][:]), r=[rpsh[which][pb]], w=[rh])
                    kb.op("pool", lambda e, h=h, which=which, c=c: e.tensor_copy(
                        out=stash[:, which, c, :], in_=h[:, 512:514]), r=[rh], w=[rstash[which][c]])
                    o = cv[which][pb]
                    ro = rcv[which][pb]
                    ci = which * NC + c
                    kb.op("dve", lambda e, h=h, o=o, ci=ci: e.tensor_scalar(
                        out=o[:], in0=h[:, 0:512], scalar1=cw[:, ci:ci + 1], scalar2=None, op0=ALU.mult),
                        r=[rh, rcw], w=[ro])
                    kb.op("dve", lambda e, h=h, o=o, ci=ci: e.scalar_tensor_tensor(
                        out=o[:], in0=h[:, 1:513], scalar=cw[:, 44 + ci:44 + ci + 1], in1=o[:],
                        op0=ALU.mult, op1=ALU.add), r=[rh, rcw, ro], w=[ro])
                    kb.op("dve", lambda e, h=h, o=o, ci=ci: e.scalar_tensor_tensor(
                        out=o[:], in0=h[:, 2:514], scalar=cw[:, 88 + ci:88 + ci + 1], in1=o[:],
                        op0=ALU.mult, op1=ALU.add), r=[rh, rcw, ro], w=[ro])
                g3, v3 = cv[0][pb], cv[1][pb]
                sg = hb[0][pb]
                kb.op("act", lambda e, g3=g3, sg=sg: e.activation(out=sg[:, 0:512], in_=g3[:], func=AF.Sigmoid),
                      r=[rcv[0][pb]], w=[rhb[0][pb]])
                kb.op("pool", lambda e, g3=g3, v3=v3: e.tensor_tensor(out=v3[:], in0=g3[:], in1=v3[:], op=ALU.mult),
                      r=[rcv[0][pb], rcv[1][pb]], w=[rcv[1][pb]])
                kb.op("pool", lambda e, sg=sg, v3=v3, c=c, blk=blk: e.tensor_tensor(
                    out=aT[:, c, blk * 512:(blk + 1) * 512], in0=sg[:, 0:512], in1=v3[:], op=ALU.mult),
                    r=[rhb[0][pb], rcv[1][pb]], w=[raT])
        for tile in range(8):
            ob = 0
            for nh in range(2):
                for c in range(NC):
                    kb.op("pe", lambda e, ob=ob, nh=nh, c=c, tile=tile: e.matmul(
                        pso[ob][:, nh * 512:(nh + 1) * 512], lhsT=aT[:, c, tile * 128:(tile + 1) * 128],
                        rhs=wd16[:, c, nh * 512:(nh + 1) * 512], start=(c == 0), stop=(c == NC - 1)),
                        r=[raT, rwd], w=[rpso[ob]])
            row = half * 1024 + tile * 128
            ep.run(pso[ob][:], rpso[ob], io["x1"][row:row + 128, :], io["out"][row:row + 128, :])
    kb.end_phase()


def load_w_bf16(kb, w16, r_w16, w_dram, ncols, chunk=512):
    kb.begin_phase()
    st = [kb.sbuf("wst", [128, 8, chunk], F32) for _ in range(2)]
    rst = [kb.res() for _ in range(2)]
    wv = w_dram.rearrange("(kc p) f -> p kc f", p=128)
    i = 0
    for c0 in range(0, ncols, chunk):
        n = min(chunk, ncols - c0)
        b = i % 2
        kb.dma("sp", st[b][:, :, 0:n], wv[:, :, c0:c0 + n], w=[rst[b]])
        cast_copy(kb, "pool" if i % 2 == 0 else "act", w16[:, :, c0:c0 + n], st[b][:, :, 0:n],
                  r=[rst[b]], w=[r_w16])
        i += 1
    kb.end_phase()


def load_colscalars(kb, cm, out, r_out, src_rows_ap, nrows):
    kb.begin_phase()
    tmp = kb.sbuf("cs_tmp", [nrows, 128], F32)
    ps = kb.psum("cs_ps", [128, nrows], F32)
    rr = kb.res()
    kb.dma("sp", tmp[:], src_rows_ap, w=[rr])
    kb.op("pe", lambda e: e.transpose(out=ps[:], in_=tmp[:], identity=cm.id32[0:nrows, 0:nrows]),
          r=[rr, cm.r_id], w=[rr])
    kb.op("dve", lambda e: e.tensor_copy(out=out, in_=ps[:]), r=[rr], w=[r_out])
    kb.end_phase()


GELU_C = 1.5957691216057308


def gelu_from(kb, src, r_src, tmp, r_tmp, out, r_out, n):
    kb.op("act", lambda e: e.activation(out=tmp, in_=src, func=AF.Square), r=[r_src], w=[r_tmp])
    kb.op("dve", lambda e: e.tensor_scalar(out=tmp, in0=tmp, scalar1=0.044715, scalar2=1.0,
                                           op0=ALU.mult, op1=ALU.add), r=[r_tmp], w=[r_tmp])
    kb.op("dve", lambda e: e.tensor_tensor(out=tmp, in0=src, in1=tmp, op=ALU.mult),
          r=[r_src, r_tmp], w=[r_tmp])
    kb.op("act", lambda e: e.activation(out=tmp, in_=tmp, func=AF.Sigmoid, scale=GELU_C),
          r=[r_tmp], w=[r_tmp])
    kb.op("dve", lambda e: e.tensor_tensor(out=out, in0=src, in1=tmp, op=ALU.mult),
          r=[r_src, r_tmp], w=[r_out])


def odd_phase(kb, cm, io):
    kb.begin_phase()
    xT = kb.sbuf("xT", [128, 8, 2 + T], BF16)
    r_xT = kb.res()
    build_xT(kb, cm, io["x"], io["halo"], xT, r_xT)
    win = kb.sbuf("win", [128, 8, 2560], BF16)
    r_win = kb.res()
    load_w_bf16(kb, win, r_win, io["w_in"], 2560)
    wo = kb.sbuf("wo", [128, 8, D], BF16)
    r_wo = kb.res()
    load_w_bf16(kb, wo, r_wo, io["w_o"], D)
    cwd = kb.sbuf("cwd", [128, 12], F32)
    r_cwd = kb.res()
    load_colscalars(kb, cm, cwd[:], r_cwd, io["conv_w"].rearrange("j (c p) -> (j c) p", p=128), 12)
    WT = kb.sbuf("WT", [128, 8, 128], BF16)
    r_WT = kb.res()
    bhi = kb.sbuf("bhi", [1, 1024], BF16)
    blo = kb.sbuf("blo", [1, 1024], BF16)
    sel = kb.sbuf("sel", [1, 2, 128], BF16)
    r_b = kb.res()
    gain = kb.sbuf("gain", [128, 512], F32)
    r_gain = kb.res()
    kb.dma("sp", gain[:], io["sgu_g"].partition_broadcast(128), w=[r_gain])
    kb.begin_phase()
    tril = kb.sbuf("tril", [128, 128], F32)
    rt = kb.res()
    kb.dma("sp", tril[:], io["tril"], w=[rt])
    wtmp = [kb.sbuf("wtmp", [128, 128], F32) for _ in range(2)]
    wt16 = [kb.sbuf("wt16", [128, 128], BF16) for _ in range(2)]
    pw = [kb.psum("pw", [128, 128], BF16) for _ in range(2)]
    rw = [kb.res() for _ in range(2)]
    for g in range(8):
        b = g % 2
        kb.dma("sp", wtmp[b][:], io["sgu_w"][g], w=[rw[b]])
        kb.op("dve", lambda e, b=b: e.tensor_tensor(out=wt16[b][:], in0=wtmp[b][:], in1=tril[:], op=ALU.mult),
              r=[rw[b], rt], w=[rw[b]])
        kb.op("pe", lambda e, b=b: e.transpose(out=pw[b][:], in_=wt16[b][:], identity=cm.id16[:]),
              r=[rw[b], cm.r_id], w=[rw[b]])
        kb.op("dve", lambda e, b=b, g=g: e.tensor_copy(out=WT[:, g, :], in_=pw[b][:]), r=[rw[b]], w=[r_WT])
    b32 = kb.sbuf("b32", [1, 1024], F32)
    b32b = kb.sbuf("b32b", [1, 1024], F32)
    kb.dma("sp", b32[:], io["sgu_b"], w=[r_b])
    kb.op("dve", lambda e: e.tensor_copy(out=bhi[:], in_=b32[:]), r=[r_b], w=[r_b])
    kb.op("dve", lambda e: e.tensor_copy(out=b32b[:], in_=bhi[:]), r=[r_b], w=[r_b])
    kb.op("dve", lambda e: e.tensor_tensor(out=b32b[:], in0=b32[:], in1=b32b[:], op=ALU.subtract), r=[r_b], w=[r_b])
    kb.op("dve", lambda e: e.tensor_copy(out=blo[:], in_=b32b[:]), r=[r_b], w=[r_b])
    kb.op("dve", lambda e: e.memset(sel[:], 0.0), r=[r_b], w=[r_b])
    kb.op("dve", lambda e: e.memset(sel[:, 0, 0:64], 1.0), r=[r_b], w=[r_b])
    kb.op("dve", lambda e: e.memset(sel[:, 1, 64:128], 1.0), r=[r_b], w=[r_b])
    kb.end_phase()

    oT = kb.sbuf("oT", [128, 8, T], BF16)
    r_oT = kb.res()
    vpad = [kb.sbuf("vpad", [128, 8, 128], BF16) for _ in range(2)]
    r_vpad = [kb.res() for _ in range(2)]
    for b in range(2):
        kb.op("pool", lambda e, b=b: e.memset(vpad[b][:], 0.0), w=[r_vpad[b]])
    psf = [kb.psum("psf", [128, 512], F32) for _ in range(3)]
    r_psf = [kb.res() for _ in range(3)]
    psv = kb.psum("psv", [128, 512], F32)
    r_psv = kb.res()
    psm = kb.psum("psm", [128, 512], F32)
    r_psm = kb.res()
    pshl = kb.psum("pshl", [128, 4], F32)
    r_pshl = kb.res()
    uT = kb.sbuf("uT", [128, 4, 512], F32)
    r_uT = [kb.res() for _ in range(4)]
    gtmp = [kb.sbuf("gtmp", [128, 512], F32) for _ in range(2)]
    r_gtmp = [kb.res() for _ in range(2)]
    ybuf = [kb.sbuf("ybuf", [128, 514], F32) for _ in range(2)]
    r_ybuf = [kb.res() for _ in range(2)]
    ysb = [kb.sbuf("ysb", [128, 512], F32) for _ in range(2)]
    r_ysb = [kb.res() for _ in range(2)]
    ystash = kb.sbuf("ystash", [128, 4, 2], F32)
    r_yst = [kb.res() for _ in range(4)]
    hl = kb.sbuf("hl", [128, 2], F32)
    vg = [kb.sbuf("vg", [128, 512], F32) for _ in range(2)]
    r_vg = [kb.res() for _ in range(2)]
    vsq = kb.sbuf("vsq", [128, 512], F32)
    sm = [kb.sbuf("sm", [128, 32], F32) for _ in range(2)]
    r_sm = [kb.res() for _ in range(2)]

    def fm_chunk(pi, col, col0, n=512):
        for kc in range(8):
            kb.op("pe", lambda e, kc=kc: e.matmul(psf[pi][:, 0:n], lhsT=win[:, kc, col:col + 128],
                                                  rhs=xT[:, kc, col0:col0 + n],
                                                  start=(kc == 0), stop=(kc == 7)),
                  r=[r_win, r_xT], w=[r_psf[pi]])

    for j in range(4):
        for w_i, colb in enumerate((1536, 2048)):
            for kc in range(8):
                kb.op("pe", lambda e, kc=kc, j=j, w_i=w_i, colb=colb: e.matmul(
                    pshl[:, 2 * w_i:2 * w_i + 2], lhsT=win[:, kc, colb + j * 128:colb + (j + 1) * 128],
                    rhs=xT[:, kc, 0:2], start=(kc == 0), stop=(kc == 7)), r=[r_win, r_xT], w=[r_pshl])
        kb.op("act", lambda e: e.copy(out=hl[:], in_=pshl[:, 0:2]), r=[r_pshl], w=[r_pshl])
        kb.op("dve", lambda e, j=j: e.tensor_tensor(out=ystash[:, j, :], in0=hl[:], in1=pshl[:, 2:4], op=ALU.mult),
              r=[r_pshl], w=[r_yst[j]])

    fi = 0
    for blk in range(4):
        col0 = 2 + blk * 512
        for j in range(4):
            p_gc, p_z, p_gb = 0, 1, 2
            fm_chunk(p_gc, 1536 + j * 128, col0)
            fm_chunk(p_z, 2048 + j * 128, col0)
            fm_chunk(p_gb, 1024 + j * 128, col0)
            yb = ybuf[j % 2]
            ryb = r_ybuf[j % 2]
            ys = ysb[j % 2]
            rys = r_ysb[j % 2]
            kb.op("act", lambda e, ys=ys: e.copy(out=ys[:], in_=psf[p_gc][:]), r=[r_psf[p_gc]], w=[rys])
            kb.op("pool", lambda e, yb=yb, j=j: e.tensor_copy(out=yb[:, 0:2], in_=ystash[:, j, :]),
                  r=[r_yst[j]], w=[ryb])
            kb.op("dve", lambda e, yb=yb, ys=ys: e.tensor_tensor(out=yb[:, 2:514], in0=ys[:], in1=psf[p_z][:],
                                                                op=ALU.mult), r=[rys, r_psf[p_z]], w=[ryb])
            kb.op("pool", lambda e, yb=yb, j=j: e.tensor_copy(out=ystash[:, j, :], in_=yb[:, 512:514]),
                  r=[ryb], w=[r_yst[j]])
            kb.op("dve", lambda e, yb=yb, ys=ys, j=j: e.tensor_scalar(
                out=ys[:], in0=yb[:, 0:512], scalar1=cwd[:, j:j + 1], scalar2=None, op0=ALU.mult),
                r=[ryb, r_cwd], w=[rys])
            kb.op("dve", lambda e, yb=yb, ys=ys, j=j: e.scalar_tensor_tensor(
                out=ys[:], in0=yb[:, 1:513], scalar=cwd[:, 4 + j:5 + j], in1=ys[:], op0=ALU.mult, op1=ALU.add),
                r=[ryb, r_cwd, rys], w=[rys])
            kb.op("dve", lambda e, yb=yb, ys=ys, j=j: e.scalar_tensor_tensor(
                out=ys[:], in0=yb[:, 2:514], scalar=cwd[:, 8 + j:9 + j], in1=ys[:], op0=ALU.mult, op1=ALU.add),
                r=[ryb, r_cwd, rys], w=[rys])
            kb.op("dve", lambda e, ys=ys, j=j, blk=blk: e.tensor_tensor(
                out=oT[:, 4 + j, blk * 512:(blk + 1) * 512], in0=ys[:], in1=psf[p_gb][:], op=ALU.mult),
                r=[rys, r_psf[p_gb]], w=[r_oT])
        for j in range(4):
            pi = j % 3
            fm_chunk(pi, j * 128, col0)
            gelu_from(kb, psf[pi][:], r_psf[pi], gtmp[j % 2][:], r_gtmp[j % 2], uT[:, j, :], r_uT[j], 512)
        for tl in range(4):
            tile = blk * 4 + tl
            tcol = 2 + tile * 128
            vb = tile % 2
            for kc in range(8):
                kb.op("pe", lambda e, kc=kc, tcol=tcol: e.matmul(
                    psv[:], lhsT=xT[:, kc, tcol:tcol + 128], rhs=win[:, kc, 512:1024],
                    start=(kc == 0), stop=(kc == 7)), r=[r_win, r_xT], w=[r_psv])
            v = vg[vb]
            rv = r_vg[vb]
            s = sm[vb]
            rs_ = r_sm[vb]
            gelu_from(kb, psv[:], r_psv, gtmp[fi % 2][:], r_gtmp[fi % 2], v[:], rv, 512)
            fi += 1
            v3 = v[:].rearrange("p (g d) -> p g d", d=64)
            kb.op("dve", lambda e, v3=v3, s=s: e.tensor_reduce(out=s[:, 0:8], in_=v3, axis=AX.X, op=ALU.add),
                  r=[rv], w=[rs_])
            kb.op("pool", lambda e, v=v: e.tensor_tensor(out=vsq[:], in0=v[:], in1=v[:], op=ALU.mult),
                  r=[rv], w=[rs_])
            kb.op("dve", lambda e, s=s: e.tensor_reduce(out=s[:, 8:16], in_=vsq[:].rearrange("p (g d) -> p g d", d=64),
                                                        axis=AX.X, op=ALU.add), r=[rs_], w=[rs_])
            kb.op("dve", lambda e, s=s: e.tensor_scalar(out=s[:, 0:8], in0=s[:, 0:8], scalar1=1.0 / 64, scalar2=None,
                                                        op0=ALU.mult), r=[rs_], w=[rs_])
            kb.op("dve", lambda e, s=s: e.tensor_tensor(out=s[:, 16:24], in0=s[:, 0:8], in1=s[:, 0:8], op=ALU.mult),
                  r=[rs_], w=[rs_])
            kb.op("dve", lambda e, s=s: e.scalar_tensor_tensor(out=s[:, 16:24], in0=s[:, 8:16], scalar=1.0 / 64,
                                                               in1=s[:, 16:24], op0=ALU.mult, op1=ALU.subtract),
                  r=[rs_], w=[rs_])
            kb.op("act", lambda e, s=s: e.activation(out=s[:, 16:24], in_=s[:, 16:24], func=AF.Sqrt, bias=LN_EPS,
                                                     scale=1.0), r=[rs_], w=[rs_])
            kb.op("dve", lambda e, s=s: e.reciprocal(out=s[:, 16:24], in_=s[:, 16:24]), r=[rs_], w=[rs_])
            for g in range(8):
                kb.op("dve", lambda e, g=g, v=v, s=s: e.tensor_scalar(
                    out=v[:, g * 64:(g + 1) * 64], in0=v[:, g * 64:(g + 1) * 64], scalar1=s[:, g:g + 1],
                    scalar2=s[:, 16 + g:17 + g], op0=ALU.subtract, op1=ALU.mult), r=[rs_, rv], w=[rv])
            vp = vpad[vb]
            rvp = r_vpad[vb]
            vp4 = vp[:].rearrange("p (gp two) c -> p gp two c", two=2)
            v4 = v[:].rearrange("p (gp two d) -> p gp two d", two=2, d=64)
            g4 = gain[:].rearrange("p (gp two d) -> p gp two d", two=2, d=64)
            kb.op("pool", lambda e, vp4=vp4, v4=v4, g4=g4: e.tensor_tensor(
                out=vp4[:, :, 0, 0:64], in0=v4[:, :, 0, :], in1=g4[:, :, 0, :], op=ALU.mult),
                r=[rv, r_gain], w=[rvp])
            kb.op("pool", lambda e, vp4=vp4, v4=v4, g4=g4: e.tensor_tensor(
                out=vp4[:, :, 1, 64:128], in0=v4[:, :, 1, :], in1=g4[:, :, 1, :], op=ALU.mult),
                r=[rv, r_gain], w=[rvp])
            for j in range(4):
                oc = psm[:, j * 128:(j + 1) * 128]
                for gi in range(2):
                    g = 2 * j + gi
                    kb.op("pe", lambda e, oc=oc, g=g, gi=gi, vp=vp: e.matmul(
                        oc, lhsT=vp[:, g, :], rhs=WT[:, g, :], start=(gi == 0), stop=False),
                        r=[rvp, r_WT], w=[r_psm])
                for gi in range(2):
                    g = 2 * j + gi
                    for bi, bt in enumerate((bhi, blo)):
                        last = (gi == 1 and bi == 1)
                        kb.op("pe", lambda e, oc=oc, g=g, gi=gi, bt=bt, last=last: e.matmul(
                            oc, lhsT=sel[:, gi, :], rhs=bt[:, g * 128:(g + 1) * 128], start=False, stop=last),
                            r=[r_b], w=[r_psm])
            kb.op("dve", lambda e, tl=tl, tile=tile: e.tensor_tensor(
                out=oT[:, 0:4, tile * 128:(tile + 1) * 128], in0=uT[:, :, tl * 128:(tl + 1) * 128],
                in1=psm[:].rearrange("p (j t) -> p j t", t=128), op=ALU.mult),
                r=r_uT + [r_psm], w=[r_oT])
    ep = LNEpilogue(kb, io["ln_g"], io["ln_b"])
    pso = kb.psum("pso", [128, D], F32)
    r_pso = kb.res()
    for tile in range(NT):
        for nh in range(2):
            for kc in range(8):
                kb.op("pe", lambda e, nh=nh, kc=kc, tile=tile: e.matmul(
                    pso[:, nh * 512:(nh + 1) * 512], lhsT=oT[:, kc, tile * 128:(tile + 1) * 128],
                    rhs=wo[:, kc, nh * 512:(nh + 1) * 512], start=(kc == 0), stop=(kc == 7)),
                    r=[r_oT, r_wo], w=[r_pso])
        row = tile * 128
        ep.run(pso[:], r_pso, io["x"][row:row + 128, :], io["out"][row:row + 128, :])
    kb.end_phase()


C_QA, C_CKV, C_QIDX, C_KIDX, C_WIDX, C_QB, C_KB, C_VB, C_FIN = 0, 512, 640, 1152, 1216, 1224, 1736, 2248, 2760
EVEN_PROJ = 2768

P1_OUT = [("qlatT", [128, 8, T], BF16), ("qidxT", [128, 4, T], BF16), ("widx", [T, 8], F32),
          ("qbT", [128, 4, T], BF16), ("ckvT", [128, T], BF16), ("kidxT", [128, T], BF16),
          ("vA", [T, 520], BF16), ("kbT", [128, 4, T], BF16), ("vB", [T, 520], BF16), ("F", [T, 8], F32)]


def even_proj_phase(kb, cm, io):
    kb.begin_phase()
    xT = kb.sbuf("xT", [128, 8, 2 + T], BF16)
    r_xT = kb.res()
    build_xT(kb, cm, io["x"], io["halo"], xT, r_xT)
    win = kb.sbuf("win", [128, 8, EVEN_PROJ], BF16)
    r_win = kb.res()
    load_w_bf16(kb, win, r_win, io["w_in"], EVEN_PROJ)
    wkd = kb.sbuf("wkd", [128, 8, 128], BF16)
    r_wkd = kb.res()
    kb.op("pool", lambda e: e.tensor_copy(out=wkd[:, :, 0:64], in_=win[:, :, C_KIDX:C_KIDX + 64]), r=[r_win], w=[r_wkd])
    kb.op("pool", lambda e: e.tensor_copy(out=wkd[:, :, 64:128], in_=win[:, :, C_KIDX:C_KIDX + 64]), r=[r_win], w=[r_wkd])
    wukT = kb.sbuf("wukT", [128, 4, 128], BF16)
    r_wuk = kb.res()
    wuv = kb.sbuf("wuv", [128, 512], BF16)
    r_wuv = kb.res()
    U32 = kb.sbuf("U32", [128, 128], F32)
    ones32 = kb.sbuf("ones32", [128, 128], F32)
    r_U = kb.res()
    kb.dma("sp", U32[:], io["triu"], w=[r_U])
    kb.dma("sp", ones32[:], io["ones"], w=[r_U])
    bfb = kb.sbuf("bfb", [128, 8], F32)
    r_bfb = kb.res()
    kb.dma("sp", bfb[:], io["b_f"].partition_broadcast(128), w=[r_bfb])
    kb.begin_phase()
    t32 = [kb.sbuf("t32", [128, 2, 64], F32) for _ in range(2)]
    rt32 = [kb.res() for _ in range(2)]
    pk = [kb.psum("pk", [128, 128], F32) for _ in range(2)]
    rpk = [kb.res() for _ in range(2)]
    for hp in range(4):
        b = hp % 2
        kb.dma("sp", t32[b][:], io["w_uk"][2 * hp:2 * hp + 2].rearrange("two r d -> r two d"), w=[rt32[b]])
        kb.op("pe", lambda e, b=b: e.transpose(out=pk[b][:], in_=t32[b][:].rearrange("r two d -> r (two d)"),
                                               identity=cm.id32[:]), r=[rt32[b], cm.r_id], w=[rpk[b]])
        kb.op("dve", lambda e, b=b, hp=hp: e.tensor_copy(out=wukT[:, hp, :], in_=pk[b][:]), r=[rpk[b]], w=[r_wuk])
    uv32 = kb.sbuf("uv32", [128, 8, 64], F32)
    ruv = kb.res()
    kb.dma("sp", uv32[:], io["w_uv"].rearrange("h r d -> r h d"), w=[ruv])
    kb.op("dve", lambda e: e.tensor_copy(out=wuv[:], in_=uv32[:].rearrange("r h d -> r (h d)")), r=[ruv], w=[r_wuv])
    kb.end_phase()

    psf = [kb.psum("psf", [128, 512], F32) for _ in range(3)]
    r_psf = [kb.res() for _ in range(3)]
    pst = [kb.psum("pst", [128, 512], F32) for _ in range(2)]
    r_pst = [kb.res() for _ in range(2)]
    pss = kb.psum("pss", [128, 16], F32)
    r_pss = kb.res()
    psF = kb.psum("psF", [128, 8], F32)
    r_psF = kb.res()
    qaT = kb.sbuf("qaT", [128, 4, 512], BF16)
    r_qaT = kb.res()
    s_qlat = kb.sbuf("s_qlat", [128, 8, 512], BF16)
    s_ckv = kb.sbuf("s_ckv", [128, 512], BF16)
    s_qidx = kb.sbuf("s_qidx", [128, 4, 512], BF16)
    s_kidx = kb.sbuf("s_kidx", [128, 512], BF16)
    s_qb = kb.sbuf("s_qb", [128, 4, 512], BF16)
    s_kb = kb.sbuf("s_kb", [128, 4, 512], BF16)
    r_s = {n: kb.res() for n in ("qlat", "ckv", "qidx", "kidx", "qb", "kb")}
    s_vA = [kb.sbuf("s_vA", [128, 8, 65], BF16) for _ in range(2)]
    s_vB = [kb.sbuf("s_vB", [128, 8, 65], BF16) for _ in range(2)]
    r_vA = [kb.res() for _ in range(2)]
    r_vB = [kb.res() for _ in range(2)]
    for b in range(2):
        kb.op("pool", lambda e, b=b: e.memset(s_vA[b][:], 1.0), w=[r_vA[b]])
        kb.op("pool", lambda e, b=b: e.memset(s_vB[b][:], 1.0), w=[r_vB[b]])
    s_widx = kb.sbuf("s_widx", [128, NT, 8], F32)
    r_widx = kb.res()
    logf = kb.sbuf("logf", [128, NT, 8], F32)
    r_logf = kb.res()
    zt = kb.sbuf("zt", [128, 8], F32)
    r_zt = kb.res()
    s_F = kb.sbuf("s_F", [128, NT, 8], F32)
    r_F = kb.res()
    cnt = [0]

    def fm(lhs_fn, col0):
        pi = cnt[0] % 3
        cnt[0] += 1
        for kc in range(8):
            kb.op("pe", lambda e, kc=kc, pi=pi: e.matmul(psf[pi][:], lhsT=lhs_fn(kc), rhs=xT[:, kc, col0:col0 + 512],
                                                         start=(kc == 0), stop=(kc == 7)),
                  r=[r_win, r_wkd, r_xT], w=[r_psf[pi]])
        return pi

    ev = [0]

    def evac(pi, dst, rdst, scale=None):
        eng = "act" if ev[0] % 2 == 0 else "dve"
        ev[0] += 1
        cast_copy(kb, eng, dst, psf[pi][:], r=[r_psf[pi]], w=[rdst], scale=scale)

    for blk in range(4):
        col0 = 2 + blk * 512
        t0 = blk * 512
        for j in range(4):
            pi = fm(lambda kc, j=j: win[:, kc, C_QA + j * 128:C_QA + (j + 1) * 128], col0)
            evac(pi, qaT[:, j, :], r_qaT)
        for h in range(8):
            pi = cnt[0] % 3
            cnt[0] += 1
            pb = (h % 2) * 64
            kb.op("pe", lambda e, pi=pi, h=h, pb=pb: e.matmul(psf[pi][:], lhsT=wukT[pb:pb + 64, h // 2, :],
                                                              rhs=qaT[pb:pb + 64, h // 2, :], start=True, stop=True),
                  r=[r_wuk, r_qaT], w=[r_psf[pi]])
            evac(pi, s_qlat[:, h, :], r_s["qlat"], scale=0.125)
        kb.dma("sp", io["qlatT"][:, :, t0:t0 + 512], s_qlat[:], r=[r_s["qlat"]])
        pi = fm(lambda kc: win[:, kc, C_CKV:C_CKV + 128], col0)
        evac(pi, s_ckv[:], r_s["ckv"])
        kb.dma("sp", io["ckvT"][:, t0:t0 + 512], s_ckv[:], r=[r_s["ckv"]])
        for j in range(4):
            pi = fm(lambda kc, j=j: win[:, kc, C_QIDX + j * 128:C_QIDX + (j + 1) * 128], col0)
            evac(pi, s_qidx[:, j, :], r_s["qidx"], scale=0.125)
        kb.dma("sp", io["qidxT"][:, :, t0:t0 + 512], s_qidx[:], r=[r_s["qidx"]])
        pi = fm(lambda kc: wkd[:, kc, :], col0)
        evac(pi, s_kidx[:], r_s["kidx"])
        kb.dma("sp", io["kidxT"][:, t0:t0 + 512], s_kidx[:], r=[r_s["kidx"]])
        for j in range(4):
            pi = fm(lambda kc, j=j: win[:, kc, C_QB + j * 128:C_QB + (j + 1) * 128], col0)
            evac(pi, s_qb[:, j, :], r_s["qb"], scale=0.125)
        kb.dma("sp", io["qbT"][:, :, t0:t0 + 512], s_qb[:], r=[r_s["qb"]])
        for j in range(4):
            pi = fm(lambda kc, j=j: win[:, kc, C_KB + j * 128:C_KB + (j + 1) * 128], col0)
            evac(pi, s_kb[:, j, :], r_s["kb"])
        kb.dma("sp", io["kbT"][:, :, t0:t0 + 512], s_kb[:], r=[r_s["kb"]])
        for tl in range(4):
            tile = blk * 4 + tl
            tcol = 2 + tile * 128
            row = tile * 128
            b = tile % 2
            kb.op("pe", lambda e, tl=tl, b=b: e.matmul(pst[0][:], lhsT=s_ckv[:, tl * 128:(tl + 1) * 128], rhs=wuv[:],
                                                       start=True, stop=True), r=[r_s["ckv"], r_wuv], w=[r_pst[0]])
            kb.op("act", lambda e, b=b: e.copy(out=s_vA[b][:, :, 0:64], in_=pst[0][:].rearrange("p (h d) -> p h d", d=64)),
                  r=[r_pst[0]], w=[r_vA[b]])
            kb.dma("sp", io["vA"][row:row + 128, :], s_vA[b][:].rearrange("p h d -> p (h d)"), r=[r_vA[b]])
            for kc in range(8):
                kb.op("pe", lambda e, kc=kc, tcol=tcol: e.matmul(pst[1][:], lhsT=xT[:, kc, tcol:tcol + 128],
                                                                 rhs=win[:, kc, C_VB:C_VB + 512],
                                                                 start=(kc == 0), stop=(kc == 7)),
                      r=[r_win, r_xT], w=[r_pst[1]])
            kb.op("dve", lambda e, b=b: e.tensor_copy(out=s_vB[b][:, :, 0:64], in_=pst[1][:].rearrange("p (h d) -> p h d", d=64)),
                  r=[r_pst[1]], w=[r_vB[b]])
            kb.dma("sp", io["vB"][row:row + 128, :], s_vB[b][:].rearrange("p h d -> p (h d)"), r=[r_vB[b]])
            for part, cc in ((0, C_FIN), (1, C_WIDX)):
                for kc in range(8):
                    kb.op("pe", lambda e, kc=kc, tcol=tcol, part=part, cc=cc: e.matmul(
                        pss[:, part * 8:(part + 1) * 8], lhsT=xT[:, kc, tcol:tcol + 128], rhs=win[:, kc, cc:cc + 8],
                        start=(kc == 0), stop=(kc == 7)), r=[r_win, r_xT], w=[r_pss])
            kb.op("dve", lambda e, tile=tile: e.tensor_scalar(out=s_widx[:, tile, :], in0=pss[:, 8:16],
                                                              scalar1=8.0 ** -0.5, scalar2=None, op0=ALU.mult),
                  r=[r_pss], w=[r_widx])
            kb.op("dve", lambda e: e.tensor_tensor(out=zt[:], in0=pss[:, 0:8], in1=bfb[:], op=ALU.add),
                  r=[r_pss, r_bfb], w=[r_zt])
            kb.op("act", lambda e: e.activation(out=zt[:], in_=zt[:], func=AF.Exp, scale=-1.0), r=[r_zt], w=[r_zt])
            kb.op("act", lambda e: e.activation(out=zt[:], in_=zt[:], func=AF.Ln, bias=1.0, scale=1.0), r=[r_zt], w=[r_zt])
            kb.op("dve", lambda e, tile=tile: e.tensor_scalar(out=logf[:, tile, :], in0=zt[:], scalar1=-1.0, scalar2=None,
                                                              op0=ALU.mult), r=[r_zt], w=[r_logf])
    kb.dma("sp", io["widx"].rearrange("(i p) h -> p i h", p=128), s_widx[:], r=[r_widx])
    for i in range(NT):
        kb.op("pe", lambda e, i=i: e.matmul(psF[:], lhsT=U32[:], rhs=logf[:, i, :], start=True, stop=(i == 0)),
              r=[r_U, r_logf], w=[r_psF])
        for i2 in range(i):
            kb.op("pe", lambda e, i=i, i2=i2: e.matmul(psF[:], lhsT=ones32[:], rhs=logf[:, i2, :], start=False,
                                                       stop=(i2 == i - 1)), r=[r_U, r_logf], w=[r_psF])
        kb.op("dve", lambda e, i=i: e.tensor_copy(out=s_F[:, i, :], in_=psF[:]), r=[r_psF], w=[r_F])
    kb.dma("sp", io["F"].rearrange("(i p) h -> p i h", p=128), s_F[:], r=[r_F])
    kb.end_phase()


def ocol(h):
    return (h // 4) * 512 + (h % 4) * 65


def dsa_phase(kb, cm, io):
    kb.begin_phase()
    ckv = kb.sbuf("ckv", [128, 2 * T], BF16)
    kidx = kb.sbuf("kidx", [128, 2 * T], BF16)
    vA = kb.sbuf("vA", [128, 32, 520], BF16)
    r_k = kb.res()
    kb.dma("sp", ckv[:, 0:T], io["p_ckvT"], w=[r_k])
    kb.dma("sp", ckv[:, T:2 * T], io["ckvT"], w=[r_k])
    kb.dma("sp", kidx[:, 0:T], io["p_kidxT"], w=[r_k])
    kb.dma("sp", kidx[:, T:2 * T], io["kidxT"], w=[r_k])
    kb.dma("sp", vA[:, 0:16, :], io["p_vA"].rearrange("(j p) c -> p j c", p=128), w=[r_k])
    kb.dma("sp", vA[:, 16:32, :], io["vA"].rearrange("(j p) c -> p j c", p=128), w=[r_k])
    pm = kb.sbuf("pm", [128, 1], F32)
    cadd = kb.sbuf("cadd", [128, 128], F32)
    biasn = kb.sbuf("biasn", [128, 2, 1024], BF16)
    b31 = kb.sbuf("b31", [1, 1024], BF16)
    I4 = kb.sbuf("I4", [128, 512], BF16)
    ones1 = kb.sbuf("ones1", [1, 128], BF16)
    r_c = kb.res()
    kb.dma("sp", pm[:], io["pm"], w=[r_c])
    kb.dma("sp", cadd[:], io["causal_add"], w=[r_c])
    kb.begin_phase()
    bn32 = kb.sbuf("bn32", [128, 1024], F32)
    rb = kb.res()
    for w_ in range(2):
        kb.dma("sp", bn32[:], io["bias_near"][w_], w=[rb])
        kb.op("dve", lambda e, w_=w_: e.tensor_copy(out=biasn[:, w_, :], in_=bn32[:]), r=[rb], w=[rb, r_c])
    b32 = kb.sbuf("b32", [1, 1024], F32)
    kb.dma("sp", b32[:], io["b31row"], w=[rb])
    kb.op("dve", lambda e: e.tensor_copy(out=b31[:], in_=b32[:]), r=[rb], w=[r_c])
    for q in range(4):
        kb.op("dve", lambda e, q=q: e.tensor_copy(out=I4[:, q * 128:(q + 1) * 128], in_=cm.id16[:]), r=[cm.r_id], w=[r_c])
    kb.op("dve", lambda e: e.memset(ones1[:], 1.0), w=[r_c])
    kb.end_phase()

    score = kb.sbuf("score", [128, 2 * T], F32)
    work = kb.sbuf("work", [128, 2 * T], F32)
    r_score = kb.res()
    r_work = kb.res()
    negm = [kb.sbuf("negm", [128, 2 * T], BF16) for _ in range(2)]
    r_negm = [kb.res() for _ in range(2)]
    rl = [kb.sbuf("rl", [128, 512], F32) for _ in range(2)]
    r_rl = [kb.res() for _ in range(2)]
    m8 = kb.sbuf("m8", [128, 8], F32)
    thr = kb.sbuf("thr", [128, 1], F32)
    r_m8 = kb.res()
    qidx = [kb.sbuf("qidx", [128, 4, 128], BF16) for _ in range(2)]
    qiz = [kb.sbuf("qiz", [128, 8, 128], BF16) for _ in range(2)]
    r_qz = [kb.res() for _ in range(2)]
    for b_ in range(2):
        kb.op("pool", lambda e, b_=b_: e.memset(qiz[b_][:], 0.0), w=[r_qz[b_]])
    qlat = [kb.sbuf("qlat", [128, 8, 128], BF16) for _ in range(2)]
    wid = [kb.sbuf("wid", [128, 8], F32) for _ in range(2)]
    r_q = [kb.res() for _ in range(2)]
    pT = [kb.sbuf("pT", [128, 1024], BF16) for _ in range(2)]
    r_pT = [kb.res() for _ in range(2)]
    oas = [kb.sbuf("oas", [128, 512], BF16) for _ in range(2)]
    r_oas = [kb.res() for _ in range(2)]
    rden = kb.sbuf("rden", [128, 8], F32)
    r_rden = kb.res()
    psI = [kb.psum("psI", [128, 512], F32) for _ in range(2)]
    r_psI = [kb.res() for _ in range(2)]
    psS = [kb.psum("psS", [128, 1024], F32) for _ in range(2)]
    r_psS = [kb.res() for _ in range(2)]
    psO = kb.psum("psO", [128, 1024], F32)
    r_psO = kb.res()
    ctr = {"i": 0, "s": 0}

    def load_q(li):
        b = li % 2
        t0 = li * 128
        kb.dma("sp", qidx[b][:], io["qidxT"][:, :, t0:t0 + 128], w=[r_q[b]])
        kb.dma("sp", qlat[b][:], io["qlatT"][:, :, t0:t0 + 128], w=[r_q[b]])
        kb.dma("sp", wid[b][:], io["widx"][t0:t0 + 128, :], w=[r_q[b]])
        qz4 = qiz[b][:].rearrange("p (hp two) t -> p hp two t", two=2)
        kb.op("pool", lambda e, b=b, qz4=qz4: e.tensor_copy(out=qz4[0:64, :, 0, :], in_=qidx[b][0:64, :, :]),
              r=[r_q[b]], w=[r_qz[b]])
        kb.op("pool", lambda e, b=b, qz4=qz4: e.tensor_copy(out=qz4[64:128, :, 1, :], in_=qidx[b][64:128, :, :]),
              r=[r_q[b]], w=[r_qz[b]])

    def idx_part(li):
        b = li % 2
        S = (16 + li + 1) * 128
        for c0 in range(0, S, 512):
            n = min(512, S - c0)
            for h in range(8):
                pi = ctr["i"] % 2
                ctr["i"] += 1
                pb = (h % 2) * 64
                kb.op("pe", lambda e, pi=pi, h=h, pb=pb, c0=c0, n=n: e.matmul(
                    psI[pi][:, 0:n], lhsT=qiz[b][:, h, :], rhs=kidx[:, c0:c0 + n],
                    start=True, stop=True), r=[r_qz[b], r_k], w=[r_psI[pi]])
                kb.op("act", lambda e, pi=pi, n=n: e.activation(out=rl[pi][:, 0:n], in_=psI[pi][:, 0:n], func=AF.Relu),
                      r=[r_psI[pi]], w=[r_rl[pi]])
                if h == 0:
                    kb.op("dve", lambda e, pi=pi, c0=c0, n=n: e.tensor_scalar(
                        out=score[:, c0:c0 + n], in0=rl[pi][:, 0:n], scalar1=wid[b][:, 0:1], scalar2=None, op0=ALU.mult),
                        r=[r_rl[pi], r_q[b]], w=[r_score])
                else:
                    kb.op("dve", lambda e, pi=pi, c0=c0, n=n, h=h: e.scalar_tensor_tensor(
                        out=score[:, c0:c0 + n], in0=rl[pi][:, 0:n], scalar=wid[b][:, h:h + 1], in1=score[:, c0:c0 + n],
                        op0=ALU.mult, op1=ALU.add), r=[r_rl[pi], r_q[b], r_score], w=[r_score])
        kb.op("dve", lambda e: e.tensor_scalar(out=score[:, 0:T], in0=score[:, 0:T], scalar1=pm[:, 0:1], scalar2=None,
                                               op0=ALU.add), r=[r_score, r_c], w=[r_score])
        kb.op("dve", lambda e: e.tensor_tensor(out=score[:, S - 128:S], in0=score[:, S - 128:S], in1=cadd[:], op=ALU.add),
              r=[r_score, r_c], w=[r_score])

    def topk_part(li):
        b = li % 2
        S = (16 + li + 1) * 128
        for rnd in range(32):
            src = score if rnd == 0 else work
            kb.op("dve", lambda e, src=src: e.max(out=m8[:], in_=src[:, 0:S]), r=[r_score, r_work], w=[r_m8])
            if rnd < 31:
                kb.op("dve", lambda e, src=src: e.match_replace(out=work[:, 0:S], in_to_replace=m8[:],
                                                                in_values=src[:, 0:S], imm_value=-3.0e38),
                      r=[r_score, r_work, r_m8], w=[r_work])
        kb.op("dve", lambda e: e.tensor_scalar(out=thr[:], in0=m8[:, 7:8], scalar1=-1.0e29, scalar2=None, op0=ALU.max),
              r=[r_m8], w=[r_m8])
        kb.op("dve", lambda e: e.tensor_scalar(out=negm[b][:, 0:S], in0=score[:, 0:S], scalar1=thr[:, 0:1], scalar2=NEG,
                                               op0=ALU.is_lt, op1=ALU.mult), r=[r_score, r_m8], w=[r_negm[b]])

    def attn_part(li):
        b = li % 2
        tiles = list(range(16)) + [16 + j for j in range(li + 1)]
        sub = 16 + li - 1
        diag = 16 + li
        for n_, jj in enumerate(tiles):
            sb = ctr["s"] % 2
            ctr["s"] += 1
            kc0 = jj * 128
            for half in range(2):
                oc = psS[sb][:, half * 512:(half + 1) * 512]
                kb.op("pe", lambda e, oc=oc, half=half, kc0=kc0: e.matmul(
                    oc, lhsT=ckv[:, kc0:kc0 + 128],
                    rhs=qlat[b][:, half * 4:(half + 1) * 4, :].rearrange("p h t -> p (h t)"),
                    start=True, stop=False), r=[r_k, r_q[b]], w=[r_psS[sb]])
                kb.op("pe", lambda e, oc=oc, kc0=kc0: e.matmul(oc, lhsT=negm[b][:, kc0:kc0 + 128], rhs=I4[:],
                                                               start=False, stop=False),
                      r=[r_negm[b], r_c], w=[r_psS[sb]])
                if jj == diag or jj == sub:
                    w_ = 0 if jj == diag else 1
                    kb.op("pe", lambda e, oc=oc, half=half, w_=w_: e.matmul(
                        oc, lhsT=cm.id16[:], rhs=biasn[:, w_, half * 512:(half + 1) * 512], start=False, stop=True),
                        r=[cm.r_id, r_c], w=[r_psS[sb]])
                else:
                    kb.op("pe", lambda e, oc=oc, half=half: e.matmul(
                        oc, lhsT=ones1[:], rhs=b31[:, half * 512:(half + 1) * 512], start=False, stop=True),
                        r=[r_c], w=[r_psS[sb]])
                kb.op("act", lambda e, oc=oc, half=half, sb=sb: e.activation(
                    out=pT[sb][:, half * 512:(half + 1) * 512], in_=oc, func=AF.Exp), r=[r_psS[sb]], w=[r_pT[sb]])
            for h in range(8):
                kb.op("pe", lambda e, h=h, sb=sb, jj=jj, n_=n_: e.matmul(
                    psO[:, ocol(h):ocol(h) + 65], lhsT=pT[sb][:, h * 128:(h + 1) * 128],
                    rhs=vA[:, jj, h * 65:(h + 1) * 65], start=(n_ == 0 and h % 4 == 0),
                    stop=(n_ == len(tiles) - 1 and h % 4 == 3)),
                    r=[r_pT[sb], r_k], w=[r_psO])
        for bk in range(2):
            den = psO[:, bk * 512:bk * 512 + 260].rearrange("p (h c) -> p h c", c=65)[:, :, 64]
            kb.op("dve", lambda e, bk=bk, den=den: e.reciprocal(out=rden[:, bk * 4:(bk + 1) * 4], in_=den),
                  r=[r_psO], w=[r_rden])
        for h in range(8):
            kb.op("dve" if h % 2 == 0 else "act", (lambda e, h=h: e.tensor_scalar(
                out=oas[b][:, h * 64:(h + 1) * 64], in0=psO[:, ocol(h):ocol(h) + 64], scalar1=rden[:, h:h + 1],
                scalar2=None, op0=ALU.mult)) if h % 2 == 0 else (lambda e, h=h: e.mul(
                    out=oas[b][:, h * 64:(h + 1) * 64], in_=psO[:, ocol(h):ocol(h) + 64], mul=rden[:, h:h + 1])),
                r=[r_psO, r_rden], w=[r_oas[b]])
        kb.dma("sp", io["oA"][li * 128:(li + 1) * 128, :], oas[b][:], r=[r_oas[b]])

    load_q(0)
    idx_part(0)
    topk_part(0)
    for li in range(NT):
        if li + 1 < NT:
            load_q(li + 1)
            idx_part(li + 1)
        attn_part(li)
        if li + 1 < NT:
            topk_part(li + 1)
    kb.end_phase()


DBG = {"fox": 9}


def fox_phase(kb, cm, io):
    kb.begin_phase()
    kbt = kb.sbuf("kbt", [128, 4, 2 * T], BF16)
    vB = kb.sbuf("vB", [128, 32, 520], BF16)
    r_k = kb.res()
    kb.dma("sp", kbt[:, :, 0:T], io["p_kbT"], w=[r_k])
    kb.dma("sp", kbt[:, :, T:2 * T], io["kbT"], w=[r_k])
    kb.dma("sp", vB[:, 0:16, :], io["p_vB"].rearrange("(j p) c -> p j c", p=128), w=[r_k])
    kb.dma("sp", vB[:, 16:32, :], io["vB"].rearrange("(j p) c -> p j c", p=128), w=[r_k])
    nFk = kb.sbuf("nFk", [128, 32, 8], F32)
    Fref = kb.sbuf("Fref", [128, NT, 8], F32)
    tot = kb.sbuf("tot", [128, 8], F32)
    pm = kb.sbuf("pm", [128, 1], F32)
    cT = kb.sbuf("cT", [128, 512], BF16)
    r_f = kb.res()
    kb.dma("sp", pm[:], io["pm"], w=[r_f])
    kb.dma("sp", nFk[:, 0:16, :], io["p_F"].rearrange("(j p) h -> p j h", p=128), w=[r_f])
    kb.dma("sp", nFk[:, 16:32, :], io["F"].rearrange("(j p) h -> p j h", p=128), w=[r_f])
    kb.dma("sp", tot[:], io["p_F"][T - 1:T, :].partition_broadcast(128), w=[r_f])
    Fv = io["F"].rearrange("(i p) h -> p i h", p=128)
    kb.dma("sp", Fref[:], Fv[64:65, :, :].partition_broadcast(128), w=[r_f])
    kb.begin_phase()
    c32 = kb.sbuf("c32", [128, 128], F32)
    rc = kb.res()
    kb.dma("sp", c32[:], io["causalT"], w=[rc])
    for q in range(4):
        kb.op("dve", lambda e, q=q: e.tensor_copy(out=cT[:, q * 128:(q + 1) * 128], in_=c32[:]), r=[rc], w=[r_f])
    kb.end_phase()
    kb.op("dve", lambda e: e.tensor_scalar(out=nFk[:], in0=nFk[:], scalar1=-1.0, scalar2=None, op0=ALU.mult),
          r=[r_f], w=[r_f])
    for j in range(16):
        kb.op("dve", lambda e, j=j: e.tensor_tensor(out=nFk[:, j, :], in0=nFk[:, j, :], in1=tot[:], op=ALU.add),
              r=[r_f], w=[r_f])
    kb.op("dve", lambda e: e.tensor_scalar(out=nFk[:, 0:16, :], in0=nFk[:, 0:16, :], scalar1=pm[:, 0:1], scalar2=None,
                                           op0=ALU.add), r=[r_f], w=[r_f])

    G = [kb.sbuf("G", [128, 8, 32], F32) for _ in range(2)]
    r_G = [kb.res() for _ in range(2)]
    qb = [kb.sbuf("qb", [128, 4, 128], BF16) for _ in range(2)]
    qbz = [kb.sbuf("qbz", [128, 8, 128], BF16) for _ in range(2)]
    r_q = [kb.res() for _ in range(2)]
    r_qz = [kb.res() for _ in range(2)]
    for b_ in range(2):
        kb.op("pool", lambda e, b_=b_: e.memset(qbz[b_][:], 0.0), w=[r_qz[b_]])
    pT = [kb.sbuf("pT", [128, 1024], BF16) for _ in range(2)]
    r_pT = [kb.res() for _ in range(2)]
    obs = [kb.sbuf("obs", [128, 512], BF16) for _ in range(2)]
    r_obs = [kb.res() for _ in range(2)]
    rden = kb.sbuf("rden", [128, 8], F32)
    r_rden = kb.res()
    psS = [kb.psum("psS", [128, 1024], F32) for _ in range(2)]
    r_psS = [kb.res() for _ in range(2)]
    psO = kb.psum("psO", [128, 1024], F32)
    r_psO = kb.res()
    ctr = {"s": 0}
    nFk_hj = nFk[:].rearrange("p j h -> p h j")

    def prep(li):
        b = li % 2
        t0 = li * 128
        kb.dma("sp", qb[b][:], io["qbT"][:, :, t0:t0 + 128], w=[r_q[b]])
        qz4 = qbz[b][:].rearrange("p (hp two) t -> p hp two t", two=2)
        kb.op("pool", lambda e, b=b, qz4=qz4: e.tensor_copy(out=qz4[0:64, :, 0, :], in_=qb[b][0:64, :, :]),
              r=[r_q[b]], w=[r_qz[b]])
        kb.op("pool", lambda e, b=b, qz4=qz4: e.tensor_copy(out=qz4[64:128, :, 1, :], in_=qb[b][64:128, :, :]),
              r=[r_q[b]], w=[r_qz[b]])
        for h in range(8):
            kb.op("pool", lambda e, h=h, b=b, li=li: e.tensor_scalar(
                out=G[b][:, h, :], in0=nFk_hj[:, h, :], scalar1=Fref[:, li, h:h + 1], scalar2=None, op0=ALU.add),
                r=[r_f], w=[r_G[b]])

    def attn(li):
        b = li % 2
        tiles = list(range(16)) + [16 + j for j in range(li + 1)]
        diag = 16 + li
        for n_, jj in enumerate(tiles):
            sb = ctr["s"] % 2
            ctr["s"] += 1
            kc0 = jj * 128
            if DBG["fox"] < 1:
                break
            for h in range(8):
                pb = (h % 2) * 64
                kb.op("pe", lambda e, h=h, pb=pb, kc0=kc0, sb=sb, jj=jj: e.matmul(
                    psS[sb][:, h * 128:(h + 1) * 128], lhsT=kbt[:, h // 2, kc0:kc0 + 128],
                    rhs=qbz[b][:, h, :], start=(h % 4 == 0), stop=(jj != diag and h % 4 == 3)),
                    r=[r_k, r_qz[b]], w=[r_psS[sb]])
            if jj == diag:
                for half in range(2):
                    kb.op("pe", lambda e, half=half, sb=sb: e.matmul(
                        psS[sb][:, half * 512:(half + 1) * 512], lhsT=cm.id16[:], rhs=cT[:], start=False, stop=True),
                        r=[cm.r_id, r_f], w=[r_psS[sb]])
            for h in range(8):
                if DBG["fox"] < 2:
                    break
                kb.op("act", lambda e, h=h, sb=sb, jj=jj: e.activation(
                    out=pT[sb][:, h * 128:(h + 1) * 128], in_=psS[sb][:, h * 128:(h + 1) * 128], func=AF.Exp,
                    bias=G[b][:, h, jj:jj + 1], scale=1.0), r=[r_psS[sb], r_G[b]], w=[r_pT[sb]])
            for h in range(8):
                if DBG["fox"] < 3:
                    break
                kb.op("pe", lambda e, h=h, sb=sb, jj=jj, n_=n_: e.matmul(
                    psO[:, ocol(h):ocol(h) + 65], lhsT=pT[sb][:, h * 128:(h + 1) * 128],
                    rhs=vB[:, jj, h * 65:(h + 1) * 65], start=(n_ == 0 and h % 4 == 0),
                    stop=(n_ == len(tiles) - 1 and h % 4 == 3)),
                    r=[r_pT[sb], r_k], w=[r_psO])
        for bk in range(2):
            den = psO[:, bk * 512:bk * 512 + 260].rearrange("p (h c) -> p h c", c=65)[:, :, 64]
            kb.op("dve", lambda e, bk=bk, den=den: e.reciprocal(out=rden[:, bk * 4:(bk + 1) * 4], in_=den),
                  r=[r_psO], w=[r_rden])
        for h in range(8):
            kb.op("dve", lambda e, h=h: e.tensor_scalar(
                out=obs[b][:, h * 64:(h + 1) * 64], in0=psO[:, ocol(h):ocol(h) + 64], scalar1=rden[:, h:h + 1],
                scalar2=None, op0=ALU.mult), r=[r_psO, r_rden], w=[r_obs[b]])
        kb.dma("sp", io["oB"][li * 128:(li + 1) * 128, :], obs[b][:], r=[r_obs[b]])

    prep(0)
    for li in range(NT):
        if li + 1 < NT:
            prep(li + 1)
        attn(li)
    kb.end_phase()


def even_out_phase(kb, cm, io):
    kb.begin_phase()
    wo = kb.sbuf("wo", [128, 8, D], BF16)
    r_wo = kb.res()
    load_w_bf16(kb, wo, r_wo, io["w_o"], D)
    ep = LNEpilogue(kb, io["ln_g"], io["ln_b"])
    ot = [kb.sbuf("ot", [128, D], BF16) for _ in range(2)]
    r_ot = [kb.res() for _ in range(2)]
    oT = [kb.sbuf("oT", [128, 8, 128], BF16) for _ in range(2)]
    r_oT = [kb.res() for _ in range(2)]
    pst = [kb.psum("pst", [128, D], BF16) for _ in range(2)]
    r_pst = [kb.res() for _ in range(2)]
    pso = [kb.psum("pso", [128, D], F32) for _ in range(2)]
    r_pso = [kb.res() for _ in range(2)]
    for tile in range(NT):
        b = tile % 2
        row = tile * 128
        kb.dma("sp", ot[b][:, 0:512], io["oA"][row:row + 128, :], w=[r_ot[b]])
        kb.dma("sp", ot[b][:, 512:1024], io["oB"][row:row + 128, :], w=[r_ot[b]])
        for k in range(8):
            kb.op("pe", lambda e, b=b, k=k: e.transpose(out=pst[b][:, k * 128:(k + 1) * 128],
                                                        in_=ot[b][:, k * 128:(k + 1) * 128], identity=cm.id16[:]),
                  r=[r_ot[b], cm.r_id], w=[r_pst[b]])
        cast_copy(kb, "act", oT[b][:], pst[b][:].rearrange("p (k t) -> p k t", t=128), r=[r_pst[b]], w=[r_oT[b]])
        for nh in range(2):
            for kc in range(8):
                kb.op("pe", lambda e, b=b, nh=nh, kc=kc: e.matmul(
                    pso[b][:, nh * 512:(nh + 1) * 512], lhsT=oT[b][:, kc, :], rhs=wo[:, kc, nh * 512:(nh + 1) * 512],
                    start=(kc == 0), stop=(kc == 7)), r=[r_oT[b], r_wo], w=[r_pso[b]])
        ep.run(pso[b][:], r_pso[b], io["x"][row:row + 128, :], io["out"][row:row + 128, :])
    kb.end_phase()


NPDT = {F32: np.float32, BF16: ml_dtypes.bfloat16}


def make_launch(phase_fns, in_specs, out_specs, scratch_specs=()):
    kb = KB()
    io = {}
    for n, shape, dt in in_specs:
        io[n] = kb.dram(n, shape, dt, kind="ExternalInput").ap()
    for n, shape, dt in out_specs:
        io[n] = kb.dram(n, shape, dt, kind="ExternalOutput").ap()
    for n, shape, dt in scratch_specs:
        io[n] = kb.dram(n, shape, dt).ap()
    cm = Common(kb, io)
    for fn in phase_fns:
        fn(kb, cm, io)
    return kb.finish()


def t5_bucket_np(n):
    n = np.maximum(n, 0)
    nf = np.maximum(n, 1).astype(np.float32)
    large = 16 + (np.log(nf / np.float32(16)) / np.float32(np.log(128 / 16)) * np.float32(16)).astype(np.int32)
    large = np.minimum(large, 31)
    return np.where(n < 16, n, large)


def host_consts(rel_bias):
    c = {}
    c["ident"] = np.eye(128, dtype=np.float32)
    c["tril"] = np.tril(np.ones((128, 128), np.float32))
    c["triu"] = np.triu(np.ones((128, 128), np.float32))
    c["ones"] = np.ones((128, 128), np.float32)
    t = np.arange(128)
    s = np.arange(128)
    c["causal_add"] = np.where(s[None, :] > t[:, None], np.float32(NEGF), np.float32(0)).astype(np.float32)
    c["causalT"] = np.where(s[:, None] > t[None, :], np.float32(NEG), np.float32(0)).astype(np.float32)
    bn = np.zeros((2, 128, 8, 128), np.float32)
    for w_ in range(2):
        dist = t[None, :] - s[:, None] + 128 * w_
        bk = t5_bucket_np(dist)
        bn[w_] = np.transpose(rel_bias[bk], (0, 2, 1))
    c["bias_near"] = bn.reshape(2, 128, 1024)
    c["b31row"] = np.repeat(rel_bias[31][:, None], 128, axis=1).reshape(1, 1024).astype(np.float32)
    return c


_PROGS = {}
NCORES = 8
PRE = {"ckvT": ([128, T], BF16), "kidxT": ([128, T], BF16), "vA": ([T, 520], BF16),
       "kbT": ([128, 4, T], BF16), "vB": ([T, 520], BF16), "F": ([T, 8], F32)}


def _f(n, s):
    return (n, list(s), F32)


def _prog(name):
    if name in _PROGS:
        return _PROGS[name]
    if name == "proj":
        ins = [_f("ident", [128, 128]), _f("x", [T, D]), _f("halo", [2, D]), _f("w_in", [D, EVEN_PROJ]),
               _f("w_uk", [8, 128, 64]), _f("w_uv", [8, 128, 64]), _f("b_f", [8]), _f("triu", [128, 128]),
               _f("ones", [128, 128])]
        nc = make_launch([even_proj_phase], ins, P1_OUT)
    elif name == "attn":
        ins = [_f("ident", [128, 128]), _f("x", [T, D]), _f("w_o", [D, D]), _f("ln_g", [D]), _f("ln_b", [D]),
               _f("pm", [128, 1]), _f("causal_add", [128, 128]), _f("causalT", [128, 128]),
               _f("bias_near", [2, 128, 1024]), _f("b31row", [1, 1024])]
        ins += list(P1_OUT)
        ins += [("p_" + n, sh, dt) for n, (sh, dt) in PRE.items()]
        nc = make_launch([dsa_phase, fox_phase, even_out_phase], ins, [("out", [T, D], F32)],
                         [("oA", [T, 512], BF16), ("oB", [T, 512], BF16)])
    elif name == "ffn":
        ins = [_f("ident", [128, 128]), _f("x1", [T, D]), _f("halo", [2, D]), _f("w_up", [D, 2 * DFF]),
               _f("conv_w", [3, 2 * DFF]), _f("w_down", [DFF, D]), _f("ln_g", [D]), _f("ln_b", [D])]
        nc = make_launch([ffn_phase], ins, [("out", [T, D], F32)])
    elif name == "odd":
        ins = [_f("ident", [128, 128]), _f("x", [T, D]), _f("halo", [2, D]), _f("w_in", [D, 2560]), _f("sgu_g", [512]),
               _f("sgu_w", [8, 128, 128]), _f("sgu_b", [1, 1024]), _f("conv_w", [3, 512]), _f("w_o", [D, D]),
               _f("ln_g", [D]), _f("ln_b", [D]), _f("tril", [128, 128])]
        nc = make_launch([odd_phase], ins, [("out", [T, D], F32)])
    _PROGS[name] = nc
    return nc


def _run(name, maps):
    res = run_bass_kernel_spmd(_prog(name), maps, core_ids=list(range(NCORES)))
    return [{k: np.asarray(v) for k, v in r.items()} for r in res.results]


def _halos(xs):
    out = []
    for c in range(NCORES):
        if c % 2 == 0:
            out.append(np.zeros((2, D), np.float32))
        else:
            out.append(np.ascontiguousarray(xs[c - 1][T - 2:T, :]))
    return out


def kernel(x, ln_g, ln_b, rel_bias, ev_w_in, ev_w_uk, ev_w_uv, ev_b_f, ev_w_o,
           od_w_in, od_sgu_g, od_sgu_w, od_sgu_b, od_conv_w, od_w_o,
           ffn_w_up, ffn_conv_w, ffn_w_down):
    a = lambda v: np.ascontiguousarray(np.asarray(v, dtype=np.float32))
    x = a(x)
    ln_g, ln_b, rel_bias = a(ln_g), a(ln_b), a(rel_bias)
    C = host_consts(rel_bias)
    xs = [np.ascontiguousarray(x[c // 2, (c % 2) * T:(c % 2 + 1) * T, :]) for c in range(NCORES)]
    for L in range(DEPTH):
        j = L // 2
        if L % 2 == 0:
            maps = [dict(ident=C["ident"], x=xs[c], halo=np.zeros((2, D), np.float32), w_in=a(ev_w_in[j]),
                         w_uk=a(ev_w_uk[j]), w_uv=a(ev_w_uv[j]), b_f=a(ev_b_f[j]), triu=C["triu"], ones=C["ones"])
                    for c in range(NCORES)]
            r1 = _run("proj", maps)
            maps = []
            for c in range(NCORES):
                m = dict(ident=C["ident"], x=xs[c], w_o=a(ev_w_o[j]), ln_g=a(ln_g[L, 0]), ln_b=a(ln_b[L, 0]),
                         causal_add=C["causal_add"], causalT=C["causalT"], bias_near=C["bias_near"],
                         b31row=C["b31row"])
                m["pm"] = np.full((128, 1), 0.0 if c % 2 == 1 else NEGF, np.float32)
                for n, _, _ in P1_OUT:
                    m[n] = r1[c][n]
                for n in PRE:
                    m["p_" + n] = r1[c - 1][n] if c % 2 == 1 else np.zeros_like(r1[c][n])
                maps.append(m)
            x1 = [r["out"] for r in _run("attn", maps)]
        else:
            hl = _halos(xs)
            maps = [dict(ident=C["ident"], x=xs[c], halo=hl[c], w_in=a(od_w_in[j]), sgu_g=a(od_sgu_g[j]).reshape(512),
                         sgu_w=a(od_sgu_w[j]), sgu_b=a(od_sgu_b[j]).reshape(1, 1024), conv_w=a(od_conv_w[j]),
                         w_o=a(od_w_o[j]), ln_g=a(ln_g[L, 0]), ln_b=a(ln_b[L, 0]), tril=C["tril"])
                    for c in range(NCORES)]
            x1 = [r["out"] for r in _run("odd", maps)]
        hl = _halos(x1)
        maps = [dict(ident=C["ident"], x1=x1[c], halo=hl[c], w_up=a(ffn_w_up[L]), conv_w=a(ffn_conv_w[L]),
                     w_down=a(ffn_w_down[L]), ln_g=a(ln_g[L, 1]), ln_b=a(ln_b[L, 1])) for c in range(NCORES)]
        xs = [r["out"] for r in _run("ffn", maps)]
    out = np.zeros((4, 2 * T, D), np.float32)
    for c in range(NCORES):
        out[c // 2, (c % 2) * T:(c % 2 + 1) * T, :] = xs[c]
    return out
```
